# Optimizing a Trainium2 kernel written in Bass

```python
import math
import jax, jax.numpy as jnp
from jax import lax
import numpy as np

D_MODEL = 2048
BATCH = 16
SEQ = 2048
DEPTH = 2

MIX_WIDTH = D_MODEL
N_MLSTM_HEADS = 4
MLSTM_HEAD_DIM = MIX_WIDTH // 2 // N_MLSTM_HEADS
MLSTM_WIDTH = N_MLSTM_HEADS * MLSTM_HEAD_DIM
N_RET_HEADS = 4
RET_HEAD_DIM = MIX_WIDTH // 2 // N_RET_HEADS
RET_WIDTH = N_RET_HEADS * RET_HEAD_DIM
CONV_WIDTH = 4
GATE_SOFTCAP = 15.0
CHUNK = 128
ROPE_BASE = 10000.0
EVEN_IN_WIDTH = 4 * MLSTM_WIDTH + 2 * N_MLSTM_HEADS + 4 * RET_WIDTH
N_ATTN_HEADS = 16
ATTN_HEAD_DIM = MIX_WIDTH // N_ATTN_HEADS
N_IDX_HEADS = 16
IDX_HEAD_DIM = 64
MAX_TOPK = 256
Q_BLOCK = 128
ODD_IN_WIDTH = (N_ATTN_HEADS * ATTN_HEAD_DIM + 2 * ATTN_HEAD_DIM
                + N_IDX_HEADS * IDX_HEAD_DIM + IDX_HEAD_DIM + N_IDX_HEADS)
REL_BUCKETS = 32
REL_MAX_DISTANCE = 128
D_FF = 4 * D_MODEL
N_EVEN = (DEPTH + 1) // 2
N_ODD = DEPTH // 2
EPS = 1e-6

kernel_name = "hybrid_mlstm_retention_dsa_trunk"


def rms_norm(x, g):
    xf = x.astype(jnp.float32)
    y = xf * lax.rsqrt(jnp.mean(xf * xf, axis=-1, keepdims=True) + EPS)
    return (y * g.astype(jnp.float32)).astype(x.dtype)


def causal_short_conv(x, w):
    k_w = w.shape[0]
    s = x.shape[1]
    xp = jnp.pad(x, ((0, 0), (k_w - 1, 0), (0, 0)))
    return sum(xp[:, j:j + s] * w[j] for j in range(k_w))


def rotary(x, pos):
    half = x.shape[-1] // 2
    inv = ROPE_BASE ** (-jnp.arange(half, dtype=jnp.float32) / half)
    ang = pos.astype(jnp.float32)[:, None] * inv[None, :]
    cos = jnp.cos(ang)[None, :, None, :]
    sin = jnp.sin(ang)[None, :, None, :]
    x1, x2 = x[..., :half], x[..., half:]
    return jnp.concatenate([x1 * cos - x2 * sin, x2 * cos + x1 * sin], axis=-1)


def to_chunks(t):
    b, h, s = t.shape[:3]
    t = t.reshape(b, h, s // CHUNK, CHUNK, *t.shape[3:])
    return jnp.moveaxis(t, 2, 0)


def from_chunks(t):
    nc, b, h, l, d = t.shape
    return jnp.moveaxis(t, 0, 2).reshape(b, h, nc * l, d)


def mlstm_chunkwise(q, k, v, log_i, log_f):
    b, h, s, dk = q.shape
    dv = v.shape[-1]
    causal = jnp.tril(jnp.ones((CHUNK, CHUNK), dtype=bool))

    def step(carry, xs):
        c_st, n_st, m_prev = carry
        qc, kc, vc, lic, lfc = xs
        bcum = jnp.cumsum(lfc, axis=-1)
        dmat = bcum[..., :, None] - bcum[..., None, :] + lic[..., None, :]
        dmat = jnp.where(causal, dmat, -jnp.inf)
        m_inter = bcum + m_prev[..., None]
        m_row = jnp.maximum(jnp.max(dmat, axis=-1), m_inter)
        sc = jnp.einsum('bhjd,bhsd->bhjs', qc, kc) * jnp.exp(dmat - m_row[..., None])
        inter = jnp.exp(m_inter - m_row)
        num = (jnp.einsum('bhjs,bhse->bhje', sc, vc)
               + inter[..., None] * jnp.einsum('bhjd,bhde->bhje', qc, c_st))
        den = sc.sum(-1) + inter * jnp.einsum('bhjd,bhd->bhj', qc, n_st)
        h_out = num / jnp.maximum(jnp.abs(den), jnp.exp(-m_row))[..., None]
        b_last = bcum[..., -1]
        g = b_last[..., None] - bcum + lic
        m_new = jnp.maximum(b_last + m_prev, jnp.max(g, axis=-1))
        w = jnp.exp(g - m_new[..., None])
        decay = jnp.exp(b_last + m_prev - m_new)
        c_new = decay[..., None, None] * c_st + jnp.einsum('bhsd,bhse->bhde', kc * w[..., None], vc)
        n_new = decay[..., None] * n_st + jnp.einsum('bhs,bhsd->bhd', w, kc)
        return (c_new, n_new, m_new), h_out

    init = (jnp.zeros((b, h, dk, dv), jnp.float32), jnp.zeros((b, h, dk), jnp.float32),
            jnp.zeros((b, h), jnp.float32))
    _, hs = lax.scan(step, init, (to_chunks(q), to_chunks(k), to_chunks(v),
                                  to_chunks(log_i), to_chunks(log_f)))
    return from_chunks(hs)


def retention_chunkwise(q, k, v, log_gamma):
    b, h, s, dk = q.shape
    dv = v.shape[-1]
    j = jnp.arange(CHUNK, dtype=jnp.float32)
    rel = j[:, None] - j[None, :]
    dmask = jnp.where(rel >= 0, jnp.exp(jnp.maximum(rel, 0.0)[None] * log_gamma[:, None, None]), 0.0)
    cross_decay = jnp.exp((j + 1.0)[None, :] * log_gamma[:, None])
    state_decay = jnp.exp((CHUNK - 1.0 - j)[None, :] * log_gamma[:, None])
    chunk_decay = jnp.exp(CHUNK * log_gamma)

    def step(r_st, xs):
        qc, kc, vc = xs
        inner = jnp.einsum('bhjs,bhse->bhje', jnp.einsum('bhjd,bhsd->bhjs', qc, kc) * dmask, vc)
        cross = jnp.einsum('bhjd,bhde->bhje', qc, r_st) * cross_decay[..., None]
        r_new = chunk_decay[:, None, None] * r_st + jnp.einsum(
            'bhsd,bhse->bhde', kc * state_decay[..., None], vc)
        return r_new, inner + cross

    init = jnp.zeros((b, h, dk, dv), jnp.float32)
    _, hs = lax.scan(step, init, (to_chunks(q), to_chunks(k), to_chunks(v)))
    return from_chunks(hs)


def mlstm_retention_mixer(h, w_in, conv_w, gate_b, head_norm_g, w_out):
    bsz, s, _ = h.shape
    dt = h.dtype
    p = h @ w_in
    cuts = np.cumsum([2 * MLSTM_WIDTH, MLSTM_WIDTH, MLSTM_WIDTH, 2 * N_MLSTM_HEADS,
                      RET_WIDTH, RET_WIDTH, RET_WIDTH]).tolist()
    qk_m, v_m, o_m, if_m, q_r, k_r, v_r, g_r = jnp.split(p, cuts, axis=-1)
    qk_m = jax.nn.silu(causal_short_conv(qk_m, conv_w))
    q_m, k_m = jnp.split(qk_m, 2, axis=-1)
    gates = if_m.astype(jnp.float32) + gate_b.astype(jnp.float32)
    gates = GATE_SOFTCAP * jnp.tanh(gates / GATE_SOFTCAP)
    log_i = jnp.transpose(gates[..., :N_MLSTM_HEADS], (0, 2, 1))
    log_f = jnp.transpose(jax.nn.log_sigmoid(gates[..., N_MLSTM_HEADS:]), (0, 2, 1))

    def heads(t, nh):
        return jnp.transpose(t.reshape(bsz, s, nh, -1), (0, 2, 1, 3)).astype(jnp.float32)

    h_m = mlstm_chunkwise(heads(q_m, N_MLSTM_HEADS),
                          heads(k_m, N_MLSTM_HEADS) * (MLSTM_HEAD_DIM ** -0.5),
                          heads(v_m, N_MLSTM_HEADS), log_i, log_f)
    pos = jnp.arange(s)
    qr = rotary(q_r.reshape(bsz, s, N_RET_HEADS, RET_HEAD_DIM).astype(jnp.float32), pos)
    kr = rotary(k_r.reshape(bsz, s, N_RET_HEADS, RET_HEAD_DIM).astype(jnp.float32), pos)
    log_gamma = jnp.log(1.0 - 2.0 ** (-5.0 - jnp.arange(N_RET_HEADS, dtype=jnp.float32)))
    h_r = retention_chunkwise(jnp.transpose(qr, (0, 2, 1, 3)),
                              jnp.transpose(kr, (0, 2, 1, 3)) * (RET_HEAD_DIM ** -0.5),
                              heads(v_r, N_RET_HEADS), log_gamma)
    y = jnp.concatenate([h_m, h_r], axis=1)
    y = y * lax.rsqrt(jnp.mean(y * y, axis=-1, keepdims=True) + EPS)
    y = y * head_norm_g.astype(jnp.float32)[None, :, None, :]
    y = jnp.transpose(y, (0, 2, 1, 3)).reshape(bsz, s, MIX_WIDTH)
    gate = jnp.concatenate([jax.nn.sigmoid(o_m.astype(jnp.float32)),
                            jax.nn.silu(g_r.astype(jnp.float32))], axis=-1)
    return (y * gate).astype(dt) @ w_out


def t5_bucket(dist):
    max_exact = REL_BUCKETS // 2
    d = jnp.maximum(dist, 0)
    large = max_exact + (jnp.log(jnp.maximum(d, 1).astype(jnp.float32) / max_exact)
                         / math.log(REL_MAX_DISTANCE / max_exact)
                         * (REL_BUCKETS - max_exact)).astype(jnp.int32)
    large = jnp.minimum(large, REL_BUCKETS - 1)
    return jnp.where(d < max_exact, d, large)


def dsa_mixer(h, w_in, q_norm_g, k_norm_g, rel_bias, w_out):
    bsz, s, _ = h.shape
    dt = h.dtype
    p = h @ w_in
    cuts = np.cumsum([N_ATTN_HEADS * ATTN_HEAD_DIM, ATTN_HEAD_DIM, ATTN_HEAD_DIM,
                      N_IDX_HEADS * IDX_HEAD_DIM, IDX_HEAD_DIM]).tolist()
    q, k_sh, v_sh, iq, ik, iw = jnp.split(p, cuts, axis=-1)
    q = rms_norm(q.reshape(bsz, s, N_ATTN_HEADS, ATTN_HEAD_DIM), q_norm_g).astype(jnp.float32)
    k_sh = rms_norm(k_sh, k_norm_g)
    kv = jnp.concatenate([k_sh, v_sh], axis=-1).astype(jnp.float32)
    iq = iq.reshape(bsz, s, N_IDX_HEADS, IDX_HEAD_DIM).astype(jnp.float32)
    ik = ik.astype(jnp.float32)
    iw = iw.astype(jnp.float32) * (N_IDX_HEADS ** -0.5)
    topk = min(MAX_TOPK, s // 4)
    nb = s // Q_BLOCK
    key_pos = jnp.arange(s)

    def blocks(t):
        return jnp.moveaxis(t.reshape(bsz, nb, Q_BLOCK, *t.shape[2:]), 1, 0)

    def attend_block(xs):
        qb, iqb, iwb, qpos = xs
        sc = jax.nn.relu(jnp.einsum('bqhd,bkd->bqhk', iqb, ik)) * (IDX_HEAD_DIM ** -0.5)
        sc = jnp.einsum('bqhk,bqh->bqk', sc, iwb)
        sc = jnp.where(key_pos[None, None, :] <= qpos[None, :, None], sc, -jnp.inf)
        _, idx = lax.top_k(sc, topk)
        valid = idx <= qpos[None, :, None]
        sel = jax.vmap(lambda a, i: a[i])(kv, idx)
        k_sel, v_sel = sel[..., :ATTN_HEAD_DIM], sel[..., ATTN_HEAD_DIM:]
        logits = jnp.einsum('bqhd,bqkd->bqhk', qb, k_sel) * (ATTN_HEAD_DIM ** -0.5)
        bias = rel_bias.astype(jnp.float32)[t5_bucket(qpos[None, :, None] - idx)]
        logits = logits + jnp.moveaxis(bias, -1, 2)
        logits = jnp.where(valid[:, :, None, :], logits, -jnp.inf)
        probs = jax.nn.softmax(logits, axis=-1)
        return jnp.einsum('bqhk,bqkd->bqhd', probs, v_sel)

    out = lax.map(attend_block, (blocks(q), blocks(iq), blocks(iw), key_pos.reshape(nb, Q_BLOCK)))
    out = jnp.moveaxis(out, 0, 1).reshape(bsz, s, N_ATTN_HEADS * ATTN_HEAD_DIM)
    return out.astype(dt) @ w_out


def sqrelu_mlp(h, w1, w2):
    return jnp.square(jax.nn.relu(h @ w1)) @ w2


def setup_inputs(seed: int = 0) -> dict:
    key = jax.random.key(seed)
    ks = jax.random.split(key, 20)
    f32 = jnp.float32
    nrm = lambda k, shp: jax.random.normal(k, shp, f32)
    gate_b = jnp.concatenate([
        0.1 * nrm(ks[10], (N_EVEN, N_MLSTM_HEADS)),
        jnp.linspace(3.0, 6.0, N_MLSTM_HEADS, dtype=f32)[None, :] + 0.1 * nrm(ks[11], (N_EVEN, N_MLSTM_HEADS)),
    ], axis=-1)
    return {
        "x": nrm(ks[0], (BATCH, SEQ, D_MODEL)),
        "c": nrm(ks[1], (BATCH, D_MODEL)),
        "ada_w": nrm(ks[2], (DEPTH, D_MODEL, 6 * D_MODEL)) * D_MODEL ** -0.5,
        "ada_b": 0.01 * nrm(ks[3], (DEPTH, 6 * D_MODEL)),
        "norm1_g": 1.0 + 0.02 * nrm(ks[4], (DEPTH, D_MODEL)),
        "norm2_g": 1.0 + 0.02 * nrm(ks[5], (DEPTH, D_MODEL)),
        "mlp_w1": nrm(ks[6], (DEPTH, D_MODEL, D_FF)) * D_MODEL ** -0.5,
        "mlp_w2": nrm(ks[7], (DEPTH, D_FF, D_MODEL)) * D_FF ** -0.5,
        "even_w_in": nrm(ks[8], (N_EVEN, D_MODEL, EVEN_IN_WIDTH)) * D_MODEL ** -0.5,
        "even_conv_w": nrm(ks[9], (N_EVEN, CONV_WIDTH, 2 * MLSTM_WIDTH)) * CONV_WIDTH ** -0.5,
        "even_gate_b": gate_b,
        "even_head_norm_g": 1.0 + 0.02 * nrm(ks[12], (N_EVEN, N_MLSTM_HEADS + N_RET_HEADS, MLSTM_HEAD_DIM)),
        "even_w_out": nrm(ks[13], (N_EVEN, MIX_WIDTH, D_MODEL)) * MIX_WIDTH ** -0.5,
        "odd_w_in": nrm(ks[14], (N_ODD, D_MODEL, ODD_IN_WIDTH)) * D_MODEL ** -0.5,
        "odd_q_norm_g": 1.0 + 0.02 * nrm(ks[15], (N_ODD, ATTN_HEAD_DIM)),
        "odd_k_norm_g": 1.0 + 0.02 * nrm(ks[16], (N_ODD, ATTN_HEAD_DIM)),
        "odd_w_out": nrm(ks[17], (N_ODD, MIX_WIDTH, D_MODEL)) * MIX_WIDTH ** -0.5,
        "rel_bias": 0.5 * nrm(ks[18], (REL_BUCKETS, N_ATTN_HEADS)),
    }


def reference(x, c, ada_w, ada_b, norm1_g, norm2_g, mlp_w1, mlp_w2,
              even_w_in, even_conv_w, even_gate_b, even_head_norm_g, even_w_out,
              odd_w_in, odd_q_norm_g, odd_k_norm_g, odd_w_out, rel_bias):
    cond = jax.nn.silu(c)
    for l in range(DEPTH):
        mod = cond @ ada_w[l] + ada_b[l]
        sh1, sc1, g1, sh2, sc2, g2 = [m[:, None, :] for m in jnp.split(mod, 6, axis=-1)]
        h = rms_norm(x, norm1_g[l]) * (1.0 + sc1) + sh1
        if l % 2 == 0:
            e = l // 2
            y = mlstm_retention_mixer(h, even_w_in[e], even_conv_w[e], even_gate_b[e],
                                      even_head_norm_g[e], even_w_out[e])
        else:
            o = l // 2
            y = dsa_mixer(h, odd_w_in[o], odd_q_norm_g[o], odd_k_norm_g[o], rel_bias, odd_w_out[o])
        x = x + g1 * y
        h = rms_norm(x, norm2_g[l]) * (1.0 + sc2) + sh2
        x = x + g2 * sqrelu_mlp(h, mlp_w1[l], mlp_w2[l])
    return x
```

```python
import numpy as np
from contextlib import ExitStack
import concourse.bass as bass
import concourse.mybir as mybir
from concourse.bass_utils import run_bass_kernel_spmd

F32 = mybir.dt.float32
BF16 = mybir.dt.bfloat16
AF = mybir.ActivationFunctionType
ALU = mybir.AluOpType
AX = mybir.AxisListType

ENGS = ['pe', 'act', 'dve', 'pool', 'sp']
SBUF_COLS = 103 * 1024


class Buf:
    __slots__ = ('name', 'last_w', 'readers', 'dsem', 'psum')

    def __init__(self, name, psum=False):
        self.name = name
        self.psum = psum
        self.last_w = None
        self.readers = []
        self.dsem = None


class V:
    __slots__ = ('ap', 'buf')

    def __init__(self, ap, buf):
        self.ap = ap
        self.buf = buf

    def __getitem__(self, idx):
        return V(self.ap[idx], self.buf)

    def bitcast(self, dt):
        return V(self.ap.bitcast(dt), self.buf)

    def rr(self, s, **kw):
        return V(self.ap.rearrange(s, **kw), self.buf)

    def bc(self, shape):
        return V(self.ap.to_broadcast(shape), self.buf)


class K:
    def __init__(self, nc, es, n_dsems=96, same_eng_sync=True):
        self.nc = nc
        self.es = es
        self.same_eng_sync = same_eng_sync
        self.big = es.enter_context(nc.sbuf_tensor("bigsb", [128, SBUF_COLS], BF16))
        self.psum = es.enter_context(nc.psum_tensor("bigps", [128, 4096], F32))
        self.bank_bufs = [Buf('bank%d' % i, psum=True) for i in range(8)]
        self.esem = {e: es.enter_context(nc.semaphore("sem_" + e)) for e in ENGS}
        self.ecount = {e: 0 for e in ENGS}
        self.dsems = [es.enter_context(nc.semaphore("dsem%d" % i)) for i in range(n_dsems)]
        self.dcount = [0] * n_dsems
        n_sw = 28
        self.dfree = {'sw': list(range(n_sw)), 'hw': list(range(n_sw, n_dsems))}
        self.phase_dsems = []
        self.ops = {e: [] for e in ENGS}
        self.known = {e: {} for e in ENGS}
        self.sb_ptr = 0
        self.sb_base = 0
        self.bufs = []
        self.nops = 0

    def sb(self, name, cols, dt=BF16, parts=128):
        n16 = cols * (2 if dt == F32 else 1)
        n16 = (n16 + 1) // 2 * 2
        assert self.sb_ptr + n16 <= SBUF_COLS, "SBUF overflow %s %d" % (name, self.sb_ptr + n16)
        ap = self.big[0:parts, self.sb_ptr:self.sb_ptr + n16]
        if dt == F32:
            ap = ap.bitcast(F32)
        if n16 != cols * (2 if dt == F32 else 1):
            ap = ap[:, 0:cols]
        self.sb_ptr += n16
        return V(ap, Buf(name))

    def ps(self, name, bank, nbanks=1, dt=F32):
        ap = self.psum[:, bank * 512:(bank + nbanks) * 512]
        if dt != F32:
            ap = ap.bitcast(dt)
        return V(ap, tuple(self.bank_bufs[bank:bank + nbanks]))

    def pss(self, bank, col0, ncols, dt=F32):
        ap = self.psum[:, bank * 512 + col0: bank * 512 + col0 + ncols]
        if dt != F32:
            ap = ap.bitcast(dt)
        return V(ap, (self.bank_bufs[bank],))

    def persist(self):
        self.sb_base = self.sb_ptr

    def _sem_for(self, key):
        if isinstance(key, str):
            return self.esem[key]
        return self.dsems[key]

    def _need(self, eng, waits, dep):
        if dep is None:
            return
        key, val = dep
        if key == eng and (eng == 'pe' or not self.same_eng_sync):
            return
        if self.known[eng].get(key, 0) >= val:
            return
        waits[key] = max(waits.get(key, 0), val)

    def op(self, eng, fn, reads=(), writes=(), dma=False):
        waits = {}
        rb = []
        wb = []
        for v in reads:
            if isinstance(v, V):
                for b in (v.buf if isinstance(v.buf, tuple) else (v.buf,)):
                    (wb if b.psum else rb).append(b)
        for v in writes:
            if isinstance(v, V):
                for b in (v.buf if isinstance(v.buf, tuple) else (v.buf,)):
                    wb.append(b)
        for b in rb:
            self._need(eng, waits, b.last_w)
        for b in wb:
            self._need(eng, waits, b.last_w)
            for r in b.readers:
                self._need(eng, waits, r)
        for k_, v_ in waits.items():
            self.known[eng][k_] = v_
        if dma:
            sb = list({id(b): b for b in (wb + rb)}.values())
            assert len(sb) == 1, "dma must touch exactly one tracked sbuf buffer"
            b = sb[0]
            cls = 'sw' if eng == 'pool' else 'hw'
            if b.dsem is None:
                b.dsem = {}
            if cls not in b.dsem:
                b.dsem[cls] = self.dfree[cls].pop()
                self.phase_dsems.append((cls, b.dsem[cls]))
            ds = b.dsem[cls]
            self.dcount[ds] += 16
            tag = (ds, self.dcount[ds])
            inc = (ds, 16)
        else:
            self.ecount[eng] += 1
            tag = (eng, self.ecount[eng])
            inc = (eng, 1)
        for b in rb:
            b.readers.append(tag)
        for b in wb:
            b.last_w = tag
            b.readers = []
        self.ops[eng].append((list(waits.items()), fn, inc))
        self.nops += 1

    def barrier(self):
        targets = [(e, self.ecount[e]) for e in ENGS if self.ecount[e] > 0]
        targets += [(d, self.dcount[d]) for d in range(len(self.dsems)) if self.dcount[d] > 0]
        for e in ENGS:
            w = []
            for key, val in targets:
                if key == e:
                    continue
                if self.known[e].get(key, 0) >= val:
                    continue
                self.known[e][key] = val
                w.append((key, val))
            if w:
                self.ops[e].append((w, None, None))
        for cls, d in self.phase_dsems:
            self.dfree[cls].append(d)
        self.phase_dsems = []
        self.sb_ptr = self.sb_base

    def emit(self):
        nc = self.nc
        with nc.Block() as block:
            def run(e, engobj):
                for waits, fn, inc in self.ops[e]:
                    for key, val in waits:
                        engobj.wait_ge(self._sem_for(key), val)
                    if fn is None:
                        continue
                    ins = fn(engobj)
                    ins.then_inc(self._sem_for(inc[0]), inc[1])

            @block.tensor
            def _(t):
                run('pe', t)

            @block.scalar
            def _(s):
                run('act', s)

            @block.vector
            def _(v):
                run('dve', v)

            @block.gpsimd
            def _(g):
                run('pool', g)

            @block.sync
            def _(s):
                run('sp', s)

    @staticmethod
    def _ap(x):
        return x.ap if isinstance(x, V) else x

    def dma(self, eng, out, in_):
        o, i = self._ap(out), self._ap(in_)
        self.op(eng, lambda E: E.dma_start(out=o, in_=i), reads=[in_], writes=[out], dma=True)

    def mm(self, out, lhsT, rhs, start=True, stop=True, skip=False):
        o, l, r = out.ap, lhsT.ap, rhs.ap
        if skip:
            self.op('pe', lambda E: E.matmul(o, l, r, start=start, stop=stop, skip_group_check=True), reads=[lhsT, rhs], writes=[out])
        else:
            self.op('pe', lambda E: E.matmul(o, l, r, start=start, stop=stop), reads=[lhsT, rhs], writes=[out])

    def tr(self, out, in_, ident):
        o, i, d = out.ap, in_.ap, ident.ap
        self.op('pe', lambda E: E.transpose(o, i, d), reads=[in_, ident], writes=[out])

    def act(self, out, in_, func, bias=None, scale=1.0, accum=None, eng='act'):
        o, i = out.ap, in_.ap
        kw = {}
        reads = [in_]
        writes = [out]
        if bias is not None:
            kw['bias'] = self._ap(bias)
            reads.append(bias)
        if isinstance(scale, V):
            kw['scale'] = scale.ap
            reads.append(scale)
        else:
            kw['scale'] = scale
        if accum is not None:
            kw['accum_out'] = accum.ap
            writes.append(accum)
        self.op('act', lambda E: E.activation(o, i, func, **kw), reads=reads, writes=writes)

    def tt(self, eng, out, in0, in1, op):
        o, a, b = out.ap, in0.ap, in1.ap
        self.op(eng, lambda E: E.tensor_tensor(o, a, b, op), reads=[in0, in1], writes=[out])

    def ts(self, eng, out, in0, s1, op0, s2=None, op1=None, accum=None):
        o, a = out.ap, in0.ap
        reads = [in0]
        writes = [out]
        if isinstance(s1, V):
            reads.append(s1)
        if isinstance(s2, V):
            reads.append(s2)
        a1, a2 = self._ap(s1), self._ap(s2)
        kw = {}
        if op1 is not None:
            kw['op1'] = op1
        if accum is not None:
            kw['accum_out'] = accum.ap
            writes.append(accum)
        self.op(eng, lambda E: E.tensor_scalar(o, a, a1, a2, op0, **kw), reads=reads, writes=writes)

    def stt(self, eng, out, in0, scalar, in1, op0, op1):
        o, a, b = out.ap, in0.ap, in1.ap
        reads = [in0, in1]
        if isinstance(scalar, V):
            reads.append(scalar)
        s = self._ap(scalar)
        self.op(eng, lambda E: E.scalar_tensor_tensor(o, a, s, b, op0, op1), reads=reads, writes=[out])

    def copy(self, eng, out, in_):
        o, i = out.ap, in_.ap
        if eng == 'act':
            self.op(eng, lambda E: E.copy(o, i), reads=[in_], writes=[out])
        else:
            self.op(eng, lambda E: E.tensor_copy(o, i), reads=[in_], writes=[out])

    def memset(self, eng, out, val):
        o = out.ap
        self.op(eng, lambda E: E.memset(o, val), reads=[], writes=[out])

    def reduce(self, eng, out, in_, op, axis=AX.X):
        o, i = out.ap, in_.ap
        self.op(eng, lambda E: E.tensor_reduce(o, i, axis, op), reads=[in_], writes=[out])

import os
SKIPW = 0
ABL_TOPK = 0
ABL_ATTN = 0
FMA_ENG = 'dve'
WQ = 'pool'
D = 2048
DFF = 8192
EPS = 1e-6


def setup_consts(k, ident_dram):
    c = {}
    c['ident'] = k.sb('ident', 128, BF16)
    k.dma('pool', c['ident'], ident_dram)
    c['identF'] = k.sb('identF', 128, F32)
    k.dma('sp', c['identF'], ident_dram)
    k.identF = c['identF']
    return c


def recip(k, eng, out, in_):
    o, i = out.ap, in_.ap
    k.op(eng, lambda E: E.reciprocal(o, i), reads=[in_], writes=[out])


def load_col(k, name, dram_row):
    r16 = k.sb(name + '_r', 128, F32, parts=16)
    k.dma('sp', r16, dram_row.rearrange("(c p) -> c p", p=128))
    pc = k.pss(7, 480, 16)
    k.mm(pc, r16, k.identF[0:16, 0:16])
    t = k.sb(name, 16, F32)
    k.copy('dve', t, pc)
    return t


def load_rowb(k, name, dram_row, n=2048):
    t = k.sb(name, n, F32)
    src = dram_row.partition_broadcast(128)
    k.dma('sp', t, src)
    return t


def norm_to_hT(k, c, xt, hT_dst, Acol, shcol, xnb, pst, ssq, rstd, junk):
    k.act(xnb if junk is None else junk, xt, AF.Square, accum=ssq)
    k.ts('dve', rstd, ssq, 1.0 / D, ALU.mult, EPS, ALU.add)
    k.act(rstd, rstd, AF.Sqrt)
    recip(k, 'dve', rstd, rstd)
    k.ts('dve', xnb, xt, rstd[:, 0:1], ALU.mult)
    for cc in range(16):
        k.tr(pst[:, cc * 128:(cc + 1) * 128], xnb[:, cc * 128:(cc + 1) * 128], c['ident'])
    for cc in range(16):
        k.act(hT_dst[cc], pst[:, cc * 128:(cc + 1) * 128], AF.Identity,
              bias=shcol[:, cc:cc + 1], scale=Acol[:, cc:cc + 1])


def phase_mlp(k, c, x_in, x_out, w1, w2, modrow, norm_g, nseq, ntok_per_seq, TB=512):
    NT = TB // 128
    gcol = load_col(k, 'gcol', norm_g)
    hT = k.sb('hT', 16 * TB, BF16)
    uT = k.sb('uT', 64 * TB, BF16)
    w1b = [k.sb('w1b%d' % i, 16 * 512, BF16) for i in range(2)]
    w2b = [k.sb('w2b%d' % i, 8 * 512, BF16) for i in range(3)]
    xts = [k.sb('xt%d' % i, 2048, F32) for i in range(2)]
    xnbs = [k.sb('xnb%d' % i, 2048, BF16) for i in range(2)]
    junk = k.sb('junk', 2048, BF16)
    ssqs = [k.sb('ssq%d' % i, 1, F32) for i in range(2)]
    rstds = [k.sb('rstd%d' % i, 1, F32) for i in range(2)]
    rt = [k.sb('rt%d' % i, 512, F32) for i in range(2)]
    xr = [k.sb('xr%d' % i, 512, F32) for i in range(3)]
    yt = [k.sb('yt%d' % i, 512, F32) for i in range(3)]
    psU = [k.ps('psU%d' % i, i) for i in range(2)]
    psY = [k.ps('psY%d' % i, 2 + i) for i in range(4)]
    psT = [k.ps('psT', 6, 2, BF16)]
    w1v = w1.rearrange("(c p) n -> p c n", p=128)
    w2v = w2.rearrange("(c p) n -> p c n", p=128)
    i_w1 = 0
    i_w2 = 0
    i_x = 0
    i_r = 0
    i_y = 0
    for s in range(nseq):
        Acol = k.sb('Acol%d' % s, 16, F32)
        sccol = load_col(k, 'sccol%d' % s, modrow[s, 4 * D:5 * D])
        shcol = load_col(k, 'shcol%d' % s, modrow[s, 3 * D:4 * D])
        G = load_rowb(k, 'G%d' % s, modrow[s, 5 * D:6 * D])
        k.stt('dve', Acol, sccol, 1.0, gcol, ALU.add, ALU.mult)
        for blk in range(ntok_per_seq // TB):
            t0 = s * ntok_per_seq + blk * TB
            for j in range(NT):
                xt = xts[i_x % 2]; xnb = xnbs[i_x % 2]; ssq = ssqs[i_x % 2]; rstd = rstds[i_x % 2]
                i_x += 1
                k.dma('sp', xt, x_in[t0 + j * 128:t0 + (j + 1) * 128, :])
                dst = [hT[:, cc * TB + j * 128: cc * TB + (j + 1) * 128] for cc in range(16)]
                norm_to_hT(k, c, xt, dst, Acol, shcol, xnb, psT[0], ssq, rstd, junk)
            for fg in range(16):
                wb = w1b[i_w1 % 2]; i_w1 += 1
                if not (SKIPW and i_w1 > 2):
                    k.dma(WQ, wb.rr("p (c n) -> p c n", c=16), w1v[:, :, fg * 512:(fg + 1) * 512])
                for fi in range(4):
                    fc = fg * 4 + fi
                    pu = psU[fc % 2]
                    for kc in range(16):
                        k.mm(pu, wb[:, kc * 512 + fi * 128: kc * 512 + (fi + 1) * 128],
                             hT[:, kc * TB:(kc + 1) * TB], start=(kc == 0), stop=(kc == 15))
                    r = rt[i_r % 2]; i_r += 1
                    k.act(r, pu, AF.Relu)
                    k.tt('dve', uT[:, fc * TB:(fc + 1) * TB], r, r, ALU.mult)
            for nb in range(4):
                for fg in range(8):
                    wb = w2b[i_w2 % 3]; i_w2 += 1
                    if not (SKIPW and i_w2 > 3):
                        k.dma(WQ, wb.rr("p (c n) -> p c n", c=8), w2v[:, fg * 8:(fg + 1) * 8, nb * 512:(nb + 1) * 512])
                    for fi in range(8):
                        fc = fg * 8 + fi
                        for j in range(NT):
                            k.mm(psY[j], uT[:, fc * TB + j * 128: fc * TB + (j + 1) * 128],
                                 wb[:, fi * 512:(fi + 1) * 512], start=(fc == 0), stop=(fc == 63))
                for j in range(NT):
                    xrt = xr[i_y % 3]; y = yt[i_y % 3]; i_y += 1
                    rows = slice(t0 + j * 128, t0 + (j + 1) * 128)
                    k.dma('sp', xrt, x_in[rows, nb * 512:(nb + 1) * 512])
                    k.tt('dve', y, psY[j], G[:, nb * 512:(nb + 1) * 512], ALU.mult)
                    k.tt('dve', y, y, xrt, ALU.add)
                    k.dma('sp', x_out[rows, nb * 512:(nb + 1) * 512], y)
    k.barrier()


def phase_mod(k, c, cvec, ada_w, ada_b, modrow, nseq, nlayers=2):
    ones = k.sb('ones', 128, BF16)
    k.memset('dve', ones, 1.0)
    condrep = []
    for s in range(nseq):
        cc = load_col(k, 'ccol%d' % s, cvec[s, :])
        cs = k.sb('csil%d' % s, 16, F32)
        k.act(cs, cc, AF.Silu)
        rep = k.sb('condrep%d' % s, 16 * 128, BF16)
        for kc in range(16):
            k.ts('dve', rep[:, kc * 128:(kc + 1) * 128], ones, cs[:, kc:kc + 1], ALU.mult)
        condrep.append(rep)
    wbs = [k.sb('adaw%d' % i, 16 * 512, BF16) for i in range(3)]
    brow = [k.sb('brow%d' % i, 512, F32, parts=1) for i in range(2)]
    orow = [k.sb('orow%d' % i, 512, F32, parts=1) for i in range(3)]
    ps = [k.ps('psm%d' % i, i) for i in range(4)]
    iw = 0
    io = 0
    for l in range(nlayers):
        wv = ada_w[l].rearrange("(c p) n -> p c n", p=128)
        for nb in range(24):
            wb = wbs[iw % 3]
            br = brow[iw % 2]
            iw += 1
            k.dma('pool', wb.rr("p (c n) -> p c n", c=16), wv[:, :, nb * 512:(nb + 1) * 512])
            k.dma('sp', br, ada_b[l:l + 1, nb * 512:(nb + 1) * 512])
            for s in range(nseq):
                p = ps[io % 4]
                o = orow[io % 3]
                io += 1
                for kc in range(16):
                    k.mm(p, condrep[s][:, kc * 128:(kc + 1) * 128], wb[:, kc * 512:(kc + 1) * 512],
                         start=(kc == 0), stop=(kc == 15))
                k.tt('dve', o, p[0:1, :], br, ALU.add)
                k.dma('sp', modrow[l, s:s + 1, nb * 512:(nb + 1) * 512], o)
    k.barrier()


def phase_outproj(k, c, yg, w_out, x_in, x_out, modrow, nseq, ntok_per_seq):
    wo = k.sb('wo', 16 * 2048, BF16)
    wv = w_out.rearrange("(c p) n -> p c n", p=128)
    for q in range(4):
        k.dma('pool', wo.rr("p (c n) -> p c n", c=16)[:, :, q * 512:(q + 1) * 512], wv[:, :, q * 512:(q + 1) * 512])
    ygt = [k.sb('ygt%d' % i, 2048, BF16) for i in range(2)]
    ygT = [k.sb('ygT%d' % i, 2048, BF16) for i in range(2)]
    xts = [k.sb('xt%d' % i, 2048, F32) for i in range(2)]
    yo = [k.sb('yo%d' % i, 2048, F32) for i in range(2)]
    psT = [k.ps('psT%d' % i, 4 + 2 * i, 2, BF16) for i in range(2)]
    psY = [k.ps('psY%d' % i, i) for i in range(4)]
    it = 0
    for s in range(nseq):
        G = load_rowb(k, 'G%d' % s, modrow[s, 2 * D:3 * D])
        for j in range(ntok_per_seq // 128):
            rows = slice(s * ntok_per_seq + j * 128, s * ntok_per_seq + (j + 1) * 128)
            a = ygt[it % 2]; aT = ygT[it % 2]; xt = xts[it % 2]; y = yo[it % 2]; pT = psT[it % 2]
            it += 1
            k.dma('sp', a, yg[rows, :])
            k.dma('sp', xt, x_in[rows, :])
            for cc in range(16):
                k.tr(pT[:, cc * 128:(cc + 1) * 128], a[:, cc * 128:(cc + 1) * 128], c['ident'])
            k.copy('act', aT, pT)
            for nb in range(4):
                p = psY[nb]
                for kc in range(16):
                    k.mm(p, aT[:, kc * 128:(kc + 1) * 128], wo[:, kc * 2048 + nb * 512: kc * 2048 + (nb + 1) * 512],
                         start=(kc == 0), stop=(kc == 15))
                k.tt('dve', y[:, nb * 512:(nb + 1) * 512], p, G[:, nb * 512:(nb + 1) * 512], ALU.mult)
            k.tt('pool', y, y, xt, ALU.add)
            k.dma('sp', x_out[rows, :], y)
    k.barrier()


def norm_seq_to_hT(k, c, x_in, row0, ntile, hT, TBW, Acol, shcol, res):
    for j in range(ntile):
        i = res['i']; res['i'] += 1
        xt = res['xt'][i % 2]; xnb = res['xnb'][i % 2]; ssq = res['ssq'][i % 2]; rstd = res['rstd'][i % 2]
        k.dma('sp', xt, x_in[row0 + j * 128: row0 + (j + 1) * 128, :])
        dst = [hT[:, cc * TBW + j * 128: cc * TBW + (j + 1) * 128] for cc in range(16)]
        norm_to_hT(k, c, xt, dst, Acol, shcol, xnb, res['psT'], ssq, rstd, res['junk'])


def norm_res(k, psbank):
    return {'i': 0,
            'xt': [k.sb('xt%d' % i, 2048, F32) for i in range(2)],
            'xnb': [k.sb('xnb%d' % i, 2048, BF16) for i in range(2)],
            'ssq': [k.sb('ssq%d' % i, 1, F32) for i in range(2)],
            'rstd': [k.sb('rstd%d' % i, 1, F32) for i in range(2)],
            'junk': None,
            'psT': k.ps('psT', psbank, 2, BF16)}


def load_AS(k, s, modrow, gcol, sec_sh, sec_sc):
    Acol = k.sb('Acol%d' % s, 16, F32)
    sccol = load_col(k, 'sccol%d' % s, modrow[s, sec_sc * D:(sec_sc + 1) * D])
    shcol = load_col(k, 'shcol%d' % s, modrow[s, sec_sh * D:(sec_sh + 1) * D])
    k.stt('dve', Acol, sccol, 1.0, gcol, ALU.add, ALU.mult)
    return Acol, shcol


def phase_projA(k, c, x_in, w_in, conv_w, modrow, norm_g, rot_cos, rot_sin, S, nseq, T=2048):
    NTB = T // 512
    NT = T // 128
    gcol = load_col(k, 'gcol', norm_g)
    cosT = k.sb('cosT', T, F32)
    sinT = k.sb('sinT', T, F32)
    k.dma('sp', cosT, rot_cos[:, 0:T])
    k.dma('sp', sinT, rot_sin[:, 0:T])
    cw = [load_col(k, 'cw%d' % j, conv_w[j, :]) for j in range(4)]
    hT = k.sb('hT', 16 * T, BF16)
    wbs = [k.sb('wb%d' % i, 16 * 512, BF16) for i in range(2)]
    wg = k.sb('wg', 16 * 8, BF16)
    res = norm_res(k, 6)
    cbuf = [k.sb('cbuf%d' % i, T + 4, F32) for i in range(2)]
    acc = [k.sb('acc%d' % i, T, F32) for i in range(1)]
    obf = [k.sb('obf%d' % i, T, BF16) for i in range(3)]
    tmp = [k.sb('tmp%d' % i, 512, F32) for i in range(4)]
    otm = [k.sb('otm%d' % i, 512, BF16) for i in range(3)]
    grow = [k.sb('grow%d' % i, T, F32, parts=4) for i in range(1)]
    psF = [k.ps('psF%d' % i, i) for i in range(4)]
    psM = [k.ps('psM%d' % i, 4 + i) for i in range(2)]
    wv = w_in.rearrange("(c p) n -> p c n", p=128)
    st = {'w': 0, 'pf': 0, 'pm': 0, 'cb': 0, 'ob': 0, 'ot': 0}

    def load_w(col0, ncols=512):
        wb = wbs[st['w'] % 2]; st['w'] += 1
        k.dma('pool', wb.rr("p (c n) -> p c n", c=16)[:, :, 0:ncols], wv[:, :, col0:col0 + ncols])
        return wb

    def fm_mm(wb, fi, tb):
        p = psF[st['pf'] % 4]; st['pf'] += 1
        for kc in range(16):
            k.mm(p, wb[:, kc * 512 + fi * 128: kc * 512 + (fi + 1) * 128],
                 hT[:, kc * T + tb * 512: kc * T + (tb + 1) * 512], start=(kc == 0), stop=(kc == 15))
        return p

    for s in range(nseq):
        Acol, shcol = load_AS(k, s, modrow, gcol, 0, 1)
        norm_seq_to_hT(k, c, x_in, s * T, NT, hT, T, Acol, shcol, res)
        k.dma('pool', wg.rr("p (c n) -> p c n", c=16), wv[:, :, 4096:4104])
        for gi in range(2):
            gr = grow[0]
            for tb in range(NTB):
                p = psF[st['pf'] % 4]; st['pf'] += 1
                for kc in range(16):
                    k.mm(p[0:4, :], wg[:, kc * 8 + gi * 4: kc * 8 + gi * 4 + 4],
                         hT[:, kc * T + tb * 512: kc * T + (tb + 1) * 512], start=(kc == 0), stop=(kc == 15))
                k.copy('act', gr[:, tb * 512:(tb + 1) * 512], p[0:4, :])
            k.dma('sp', S['gates'][s, gi * 4:(gi + 1) * 4, 0:T], gr)
        for g in range(4):
            wb = load_w(g * 512)
            for fi in range(4):
                ch = g * 4 + fi
                cb = cbuf[st['cb'] % 2]; ac = acc[0]; st['cb'] += 1
                ob = obf[st['ob'] % 3]; st['ob'] += 1
                k.memset('pool', cb[:, 0:3], 0.0)
                for tb in range(NTB):
                    p = fm_mm(wb, fi, tb)
                    k.copy('act', cb[:, 3 + tb * 512: 3 + (tb + 1) * 512], p)
                k.ts('dve', ac, cb[:, 0:T], cw[0][:, ch:ch + 1], ALU.mult)
                for j in range(1, 4):
                    k.stt('dve', ac, cb[:, j:j + T], cw[j][:, ch:ch + 1], ac, ALU.mult, ALU.add)
                k.act(ob, ac, AF.Silu)
                k.dma('sp', S['qkT_m'][s, ch * 128:(ch + 1) * 128, 0:T], ob)
        for which, col0, dst in ((0, 4104, S['qT_r']), (1, 5128, S['kT_r'])):
            for g in range(2):
                wb = load_w(col0 + g * 512)
                for hh in range(2):
                    head = g * 2 + hh
                    o1 = obf[st['ob'] % 3]; st['ob'] += 1
                    o2 = obf[st['ob'] % 3]; st['ob'] += 1
                    for tb in range(NTB):
                        p1 = fm_mm(wb, hh * 2, tb)
                        p2 = fm_mm(wb, hh * 2 + 1, tb)
                        cs = cosT[:, tb * 512:(tb + 1) * 512]
                        sn = sinT[:, tb * 512:(tb + 1) * 512]
                        k.tt('dve', tmp[0], p1, cs, ALU.mult)
                        k.tt('dve', tmp[1], p2, sn, ALU.mult)
                        k.tt('dve', tmp[2], p2, cs, ALU.mult)
                        k.tt('dve', tmp[3], p1, sn, ALU.mult)
                        k.tt('pool', o1[:, tb * 512:(tb + 1) * 512], tmp[0], tmp[1], ALU.subtract)
                        k.tt('pool', o2[:, tb * 512:(tb + 1) * 512], tmp[2], tmp[3], ALU.add)
                    k.dma('sp', dst[s, head * 256:head * 256 + 128, 0:T], o1)
                    k.dma('sp', dst[s, head * 256 + 128:head * 256 + 256, 0:T], o2)
        for col0, dst, fn in ((2048, S['v_m'], None), (3072, S['og'], AF.Sigmoid),
                              (6152, S['v_r'], None), (7176, S['gg'], AF.Silu)):
            for g in range(2):
                wb = load_w(col0 + g * 512)
                for j in range(NT):
                    p = psM[st['pm'] % 2]; st['pm'] += 1
                    o = otm[st['ot'] % 3]; st['ot'] += 1
                    for kc in range(16):
                        k.mm(p, hT[:, kc * T + j * 128: kc * T + (j + 1) * 128], wb[:, kc * 512:(kc + 1) * 512],
                             start=(kc == 0), stop=(kc == 15))
                    if fn is None:
                        k.copy('act', o, p)
                    else:
                        k.act(o, p, fn)
                    k.dma('sp', dst[s, j * 128:(j + 1) * 128, g * 512:(g + 1) * 512], o)
    k.barrier()


GAMMAS = [1.0 - 2.0 ** (-5.0 - r) for r in range(4)]


def host_consts_B():
    s = np.arange(128)[:, None].astype(np.float64)
    j = np.arange(128)[None, :].astype(np.float64)
    maskc = np.zeros((5, 128, 128), np.float64)
    maskc[0] = (s <= j) / 16.0
    for r, g in enumerate(GAMMAS):
        maskc[1 + r] = (s <= j) * (g ** (-s - 1.0)) / 16.0
    retc = np.zeros((128, 8), np.float64)
    for r, g in enumerate(GAMMAS):
        retc[:, r] = g ** (np.arange(128) + 1.0)
        retc[:, 4 + r] = g ** (127.0 - np.arange(128)) / 16.0
    return maskc.astype(np.float32), retc.astype(np.float32)


def scan(k, eng, out, d0, d1, initial, op0, op1):
    o, a, b = out.ap, d0.ap, d1.ap
    k.op(eng, lambda E: E.tensor_tensor_scan(o, a, b, initial, op0, op1), reads=[d0, d1], writes=[out])


def phase_gatesB(k, c, S, gate_b, G, nseq, T=2048):
    NR = 4 * nseq
    NC = T // 128
    identF = G['identF']
    pi = k.sb('pi', T, F32, parts=NR)
    pf = k.sb('pf', T, F32, parts=NR)
    gbi = k.sb('gbi', 1, F32, parts=NR)
    gbf = k.sb('gbf', 1, F32, parts=NR)
    for s in range(nseq):
        k.dma('sp', pi[s * 4:(s + 1) * 4, :], S['gates'][s, 0:4, 0:T])
        k.dma('sp', pf[s * 4:(s + 1) * 4, :], S['gates'][s, 4:8, 0:T])
        k.dma('sp', gbi[s * 4:(s + 1) * 4, :], gate_b[0:4].rearrange("(p o) -> p o", o=1))
        k.dma('sp', gbf[s * 4:(s + 1) * 4, :], gate_b[4:8].rearrange("(p o) -> p o", o=1))
    k.ts('dve', gbi, gbi, 1.0 / 15.0, ALU.mult)
    k.ts('dve', gbf, gbf, 1.0 / 15.0, ALU.mult)
    li = k.sb('li', T, F32, parts=NR)
    lf = k.sb('lf', T, F32, parts=NR)
    k.act(li, pi, AF.Tanh, bias=gbi, scale=1.0 / 15.0)
    k.ts('dve', li, li, 15.0, ALU.mult)
    k.act(lf, pf, AF.Tanh, bias=gbf, scale=1.0 / 15.0)
    k.act(lf, lf, AF.Exp, scale=-15.0)
    k.ts('dve', lf, lf, 1.0, ALU.add)
    k.act(lf, lf, AF.Ln)
    k.ts('dve', lf, lf, -1.0, ALU.mult)
    rm = k.sb('rm', T, F32, parts=NR)
    k.memset('dve', rm, 1.0)
    k.memset('dve', rm.rr("p (c t) -> p c t", t=128)[:, :, 0:1], 0.0)
    b = k.sb('b', T, F32, parts=NR)
    scan(k, 'dve', b, rm, lf, 0.0, ALU.mult, ALU.add)
    a = pi
    k.tt('dve', a, li, b, ALU.subtract)
    amax = k.sb('amax', NC, F32, parts=NR)
    k.reduce('dve', amax, a.rr("p (c t) -> p c t", t=128), ALU.max)
    blast = k.sb('blast', NC, F32, parts=NR)
    k.copy('dve', blast, b.rr("p (c t) -> p c t", t=128)[:, :, 127])
    mnext = k.sb('mnext', NC, F32, parts=NR)
    scan(k, 'dve', mnext, amax, blast, 0.0, ALU.max, ALU.add)
    mm_ = k.sb('mm', NC, F32, parts=NR)
    k.memset('dve', mm_[:, 0:1], 0.0)
    k.copy('dve', mm_[:, 1:NC], mnext[:, 0:NC - 1])
    mu = k.sb('mu', NC, F32, parts=NR)
    k.tt('dve', mu, mm_, amax, ALU.max)
    kap = k.sb('kap', NC, F32, parts=NR)
    k.tt('dve', kap, mm_, mu, ALU.subtract)
    k.act(kap, kap, AF.Exp)
    ea = pf
    xi = li
    for cc in range(NC):
        sl = slice(cc * 128, (cc + 1) * 128)
        k.ts('dve', ea[:, sl], a[:, sl], mu[:, cc:cc + 1], ALU.subtract)
        k.ts('dve', xi[:, sl], b[:, sl], mu[:, cc:cc + 1], ALU.add, -1.0, ALU.mult)
    k.act(ea, ea, AF.Exp)
    k.act(xi, xi, AF.Exp)
    pe_ = k.ps('pse', 0)
    px_ = k.ps('psx', 1)
    pk_ = k.ps('psk', 2)
    for cc in range(NC):
        sl = slice(cc * 128, (cc + 1) * 128)
        k.mm(pe_[:, cc * NR:(cc + 1) * NR], ea[:, sl], identF[0:NR, 0:NR])
        k.mm(px_[:, cc * NR:(cc + 1) * NR], xi[:, sl], identF[0:NR, 0:NR])
    k.copy('dve', G['ea_col'], pe_[:, 0:NC * NR])
    k.copy('dve', G['xi_col'], px_[:, 0:NC * NR])
    Rm = k.sb('Rm', NC * NR, F32, parts=NR)
    Rv = Rm.rr("p (c q) -> p c q", q=NR)
    for p in range(NR):
        k.ts('dve', Rv[:, :, p], kap, identF[0:NR, p:p + 1], ALU.mult)
    onesF = k.sb('onesF', 128, F32, parts=NR)
    k.memset('dve', onesF, 1.0)
    k.mm(pk_[:, 0:NC * NR], onesF, Rm)
    k.copy('dve', G['kap_col'], pk_[:, 0:NC * NR])
    if 'dbg' in S:
        k.dma('sp', S['dbg'][0], li if False else a)
        k.dma('sp', S['dbg'][1], b)
    k.barrier()


def phase_recB(k, c, S, G, head_norm_g, maskc_d, retc_d, nseq, T=2048):
    NR = 4 * nseq
    NC = T // 128
    ident = c['ident']
    maskc = []
    for i in range(5):
        t = k.sb('maskc%d' % i, 128, F32)
        k.dma('sp', t, maskc_d[i])
        maskc.append(t)
    retc = k.sb('retc', 8, F32)
    k.dma('sp', retc, retc_d)
    gn = [load_rowb(k, 'gn%d' % h, head_norm_g[h, :], n=256) for h in range(8)]
    GH = 2
    qTs = [k.sb('qT%d' % i, 2 * T, BF16) for i in range(GH)]
    kTs = [k.sb('kT%d' % i, 2 * T, BF16) for i in range(GH)]
    Vs = [k.sb('V%d' % i, NC * 257, BF16) for i in range(GH)]
    ogs = [k.sb('og%d' % i, NC * 256, BF16) for i in range(GH)]
    ygs = [k.sb('yg%d' % i, NC * 256, BF16) for i in range(GH)]
    Ss = [k.sb('S%d' % i, 514, F32) for i in range(GH)]
    Sbs = [k.sb('Sb%d' % i, 514, BF16) for i in range(GH)]
    scts = [k.sb('sct%d' % i, 128, BF16) for i in range(3)]
    kts = [k.sb('kt%d' % i, 256, BF16) for i in range(3)]
    hhs = [k.sb('hh%d' % i, 256, F32) for i in range(3)]
    y32s = [k.sb('y32%d' % i, 256, F32) for i in range(3)]
    jnk = [k.sb('jnk%d' % i, 256, BF16) for i in range(2)]
    dns = [k.sb('dn%d' % i, 1, F32) for i in range(4)]
    sss = [k.sb('ss%d' % i, 1, F32) for i in range(4)]
    psN = [k.pss(3 * i, 0, 257) for i in range(2)]
    psST = [k.pss(3 * i, 384, 128) for i in range(2)]
    psKV = [k.pss(3 * (i // 2) + 1 + (i % 2), 0, 257) for i in range(4)]
    psK = [k.pss(3 * i + 1, 384, 128, BF16) for i in range(2)]
    it = 0
    for s in range(nseq):
        for grp in range(4):
            heads = [grp * GH + i for i in range(GH)]
            for gi, hd in enumerate(heads):
                ml = hd < 4
                r = hd - 4
                if ml:
                    qsrc = S['qkT_m'][s, hd * 256:(hd + 1) * 256, 0:T]
                    ksrc = S['qkT_m'][s, 1024 + hd * 256:1024 + (hd + 1) * 256, 0:T]
                    vsrc = S['v_m'][s, 0:T, hd * 256:(hd + 1) * 256]
                    osrc = S['og'][s, 0:T, hd * 256:(hd + 1) * 256]
                else:
                    qsrc = S['qT_r'][s, r * 256:(r + 1) * 256, 0:T]
                    ksrc = S['kT_r'][s, r * 256:(r + 1) * 256, 0:T]
                    vsrc = S['v_r'][s, 0:T, r * 256:(r + 1) * 256]
                    osrc = S['gg'][s, 0:T, r * 256:(r + 1) * 256]
                k.dma('sp', qTs[gi].rr("p (a t) -> p a t", a=2), qsrc.rearrange("(a p) t -> p a t", p=128))
                k.dma('sp', kTs[gi].rr("p (a t) -> p a t", a=2), ksrc.rearrange("(a p) t -> p a t", p=128))
                Vv = Vs[gi].rr("p (c e) -> p c e", e=257)
                k.dma('sp', Vv[:, :, 0:256], vsrc.rearrange("(c p) e -> p c e", p=128))
                k.dma('sp', ogs[gi].rr("p (c e) -> p c e", e=256), osrc.rearrange("(c p) e -> p c e", p=128))
                if ml:
                    k.memset('pool', Vv[:, :, 256:257], 1.0)
                    for cc in range(NC):
                        col = cc * NR + s * 4 + hd
                        k.ts('pool', Vs[gi][:, cc * 257:(cc + 1) * 257], Vs[gi][:, cc * 257:(cc + 1) * 257],
                             G['ea_col'][:, col:col + 1], ALU.mult)
            for cc in range(NC):
                ch = slice(cc * 128, (cc + 1) * 128)
                for gi, hd in enumerate(heads):
                    ml = hd < 4
                    r = hd - 4
                    W = 257 if ml else 256
                    qT, kT, Vt, og, yg_, St, Sb = qTs[gi], kTs[gi], Vs[gi], ogs[gi], ygs[gi], Ss[gi], Sbs[gi]
                    col = cc * NR + s * 4 + hd
                    q0 = qT[:, cc * 128:(cc + 1) * 128]
                    q1 = qT[:, T + cc * 128:T + (cc + 1) * 128]
                    k0 = kT[:, cc * 128:(cc + 1) * 128]
                    k1 = kT[:, T + cc * 128:T + (cc + 1) * 128]
                    Vc = Vt[:, cc * 257:cc * 257 + W]
                    st_ = psST[it % 2]; pk = psK[it % 2]; pn = psN[it % 2]
                    kv0 = psKV[(it % 2) * 2]; kv1 = psKV[(it % 2) * 2 + 1]
                    sct = scts[it % 3]; kt = kts[it % 3]; hh = hhs[it % 3]; y32 = y32s[it % 3]
                    dn = dns[it % 4]; ss = sss[it % 4]; jk = jnk[it % 2]
                    it += 1
                    k.mm(st_, k0, q0, start=True, stop=False)
                    k.mm(st_, k1, q1, start=False, stop=True)
                    k.tt('dve', sct, st_, maskc[0 if ml else 1 + r], ALU.mult)
                    k.tr(pk[:, 0:128], k0, ident)
                    k.tr(pk[:, 128:256], k1, ident)
                    if ml:
                        k.act(kt, pk, AF.Identity, scale=1.0 / 16.0)
                    else:
                        k.act(kt, pk, AF.Identity, scale=retc[:, 4 + r:5 + r])
                    k.mm(pn[:, 0:W], sct, Vc, start=True, stop=(cc == 0))
                    if cc > 0:
                        k.mm(pn[:, 0:W], q0, Sb[:, 0:W], start=False, stop=False)
                        k.mm(pn[:, 0:W], q1, Sb[:, 257:257 + W], start=False, stop=True)
                    k.mm(kv0[:, 0:W], kt[:, 0:128], Vc, start=True, stop=True)
                    k.mm(kv1[:, 0:W], kt[:, 128:256], Vc, start=True, stop=True)
                    if cc < NC - 1:
                        for dc, kv in ((0, kv0), (1, kv1)):
                            Sd = St[:, dc * 257:dc * 257 + W]
                            if cc == 0:
                                k.copy('dve', Sd, kv[:, 0:W])
                            elif ml:
                                k.stt('dve', Sd, Sd, G['kap_col'][:, col:col + 1], kv[:, 0:W], ALU.mult, ALU.add)
                            else:
                                k.stt('dve', Sd, Sd, float(GAMMAS[r] ** 128), kv[:, 0:W], ALU.mult, ALU.add)
                        if ml:
                            coln = (cc + 1) * NR + s * 4 + hd
                            k.act(Sb, St, AF.Identity, scale=G["kap_col"][:, coln:coln + 1])
                        else:
                            k.act(Sb, St, AF.Copy)
                    if ml:
                        k.act(dn, pn[:, 256:257], AF.Abs)
                        k.ts('dve', dn, dn, G['xi_col'][:, col:col + 1], ALU.max)
                        recip(k, 'dve', dn, dn)
                        k.ts('dve', hh, pn[:, 0:256], dn[:, 0:1], ALU.mult)
                    else:
                        k.ts('dve', hh, pn[:, 0:256], retc[:, r:r + 1], ALU.mult)
                    k.act(jk, hh, AF.Square, accum=ss)
                    k.ts('dve', ss, ss, 1.0 / 256.0, ALU.mult, EPS, ALU.add)
                    k.act(ss, ss, AF.Sqrt)
                    recip(k, 'dve', ss, ss)
                    k.stt('dve', y32, hh, ss[:, 0:1], gn[hd], ALU.mult, ALU.mult)
                    k.tt('pool', yg_[:, cc * 256:(cc + 1) * 256], y32, og[:, cc * 256:(cc + 1) * 256], ALU.mult)
            for gi, hd in enumerate(heads):
                k.dma('sp', S['yg'][s, 0:T, hd * 256:(hd + 1) * 256].rearrange("(c p) e -> p c e", p=128),
                      ygs[gi].rr("p (c e) -> p c e", e=256))
    k.barrier()


def phase_projE(k, c, x_in, w_in, modrow, norm_g, qng, kng, S, nseq, T=2048):
    NTB = T // 512
    NT = T // 128
    ident = c['ident']
    gcol = load_col(k, 'gcol', norm_g)
    gq = load_rowb(k, 'gq', qng, n=128)
    gk = load_rowb(k, 'gk', kng, n=128)
    hT = k.sb('hT', 16 * T, BF16)
    wbs = [k.sb('wb%d' % i, 16 * 512, BF16) for i in range(2)]
    res = norm_res(k, 6)
    qs = [k.sb('qs%d' % i, 512, F32) for i in range(2)]
    sq = [k.sb('sq%d' % i, 512, F32) for i in range(2)]
    ssh = [k.sb('ssh%d' % i, 4, F32) for i in range(3)]
    qn = [k.sb('qn%d' % i, 512, BF16) for i in range(2)]
    qTo = [k.sb('qTo%d' % i, 512, BF16) for i in range(3)]
    vo = [k.sb('vo%d' % i, 128, BF16) for i in range(3)]
    iwo = [k.sb('iwo%d' % i, 16, F32) for i in range(3)]
    obf = [k.sb('obf%d' % i, T, BF16) for i in range(3)]
    psF = [k.ps('psF%d' % i, i) for i in range(2)]
    psM = [k.ps('psM%d' % i, 2 + i) for i in range(2)]
    psQ = [k.ps('psQ%d' % i, 4 + i, 1, BF16) for i in range(2)]
    wv = w_in.rearrange("(c p) n -> p c n", p=128)
    st = {'w': 0, 'pf': 0, 'pm': 0, 'q': 0, 'ob': 0, 'o3': 0}

    def rms_heads(src, nh, ss):
        sqt = sq[st['q'] % 2]
        k.tt('pool', sqt[:, 0:nh * 128], src, src, ALU.mult)
        k.reduce('dve', ss[:, 0:nh], sqt[:, 0:nh * 128].rr("p (h d) -> p h d", d=128), ALU.add)
        k.ts('dve', ss[:, 0:nh], ss[:, 0:nh], 1.0 / 128.0, ALU.mult, EPS, ALU.add)
        k.act(ss[:, 0:nh], ss[:, 0:nh], AF.Sqrt)
        recip(k, 'dve', ss[:, 0:nh], ss[:, 0:nh])

    for s in range(nseq):
        Acol, shcol = load_AS(k, s, modrow, gcol, 0, 1)
        norm_seq_to_hT(k, c, x_in, s * T, NT, hT, T, Acol, shcol, res)
        items = [(g, j) for g in range(4) for j in range(NT)]
        wb_of = {}

        def qfront(g, j):
            if g not in wb_of:
                wb = wbs[st['w'] % 2]; st['w'] += 1
                k.dma('pool', wb.rr("p (c n) -> p c n", c=16), wv[:, :, g * 512:(g + 1) * 512])
                wb_of[g] = wb
            wb = wb_of[g]
            p = psM[st['pm'] % 2]; st['pm'] += 1
            for kc in range(16):
                k.mm(p, hT[:, kc * T + j * 128: kc * T + (j + 1) * 128], wb[:, kc * 512:(kc + 1) * 512],
                     start=(kc == 0), stop=(kc == 15))
            return p

        nxt = qfront(*items[0])
        for idx, (g, j) in enumerate(items):
            p = nxt
            q_ = qs[st['q'] % 2]; qn_ = qn[st['q'] % 2]; ss = ssh[st['o3'] % 3]; pq = psQ[st['q'] % 2]
            qo = qTo[st['o3'] % 3]
            k.copy('act', q_, p)
            if idx + 1 < len(items):
                nxt = qfront(*items[idx + 1])
            rms_heads(q_, 4, ss)
            st['q'] += 1; st['o3'] += 1
            for h in range(4):
                k.stt('dve', qn_[:, h * 128:(h + 1) * 128], q_[:, h * 128:(h + 1) * 128], ss[:, h:h + 1], gq,
                      ALU.mult, ALU.mult)
            for h in range(4):
                k.tr(pq[:, h * 128:(h + 1) * 128], qn_[:, h * 128:(h + 1) * 128], ident)
            k.copy('act', qo, pq[:, 0:512])
            k.dma('sp', S['qT_a'][s, j, :, g * 512:(g + 1) * 512], qo)
        wb = wbs[st['w'] % 2]; st['w'] += 1
        wbv = wb.rr("p (c n) -> p c n", c=16)
        k.dma('pool', wbv[:, :, 0:256], wv[:, :, 2048:2304])
        k.dma('pool', wbv[:, :, 256:272], wv[:, :, 3392:3408])
        kTo = obf[st['ob'] % 3]; st['ob'] += 1
        for j in range(NT):
            p = psM[st['pm'] % 2]; st['pm'] += 1
            for kc in range(16):
                k.mm(p[:, 0:272], hT[:, kc * T + j * 128: kc * T + (j + 1) * 128], wb[:, kc * 512:kc * 512 + 272],
                     start=(kc == 0), stop=(kc == 15))
            q_ = qs[st['q'] % 2]; qn_ = qn[st['q'] % 2]; ss = ssh[st['o3'] % 3]; pq = psQ[st['q'] % 2]
            v_ = vo[st['o3'] % 3]; iw_ = iwo[st['o3'] % 3]
            k.copy('act', q_[:, 0:272], p[:, 0:272])
            rms_heads(q_[:, 0:128], 1, ss)
            st['q'] += 1; st['o3'] += 1
            k.stt('dve', qn_[:, 0:128], q_[:, 0:128], ss[:, 0:1], gk, ALU.mult, ALU.mult)
            k.tr(pq[:, 0:128], qn_[:, 0:128], ident)
            k.copy('act', kTo[:, j * 128:(j + 1) * 128], pq[:, 0:128])
            k.copy('pool', v_, q_[:, 128:256])
            k.copy('pool', iw_, q_[:, 256:272])
            k.dma('sp', S['v_a'][s, j * 128:(j + 1) * 128, :], v_)
            k.dma('sp', S['iw_a'][s, j * 128:(j + 1) * 128, :], iw_)
        k.dma('sp', S['kT_a'][s, :, 0:T], kTo)
        for g in range(3):
            wb = wbs[st['w'] % 2]; st['w'] += 1
            wbv = wb.rr("p (c n) -> p c n", c=16)
            if g < 2:
                k.dma('pool', wbv, wv[:, :, 2304 + g * 512: 2304 + (g + 1) * 512])
                nch = 4
            else:
                k.dma('pool', wbv[:, :, 0:64], wv[:, :, 3328:3392])
                k.dma('pool', wbv[:, :, 64:128], wv[:, :, 3328:3392])
                nch = 1
            for fi in range(nch):
                ob = obf[st['ob'] % 3]; st['ob'] += 1
                for tb in range(NTB):
                    p = psF[st['pf'] % 2]; st['pf'] += 1
                    for kc in range(16):
                        k.mm(p, wb[:, kc * 512 + fi * 128: kc * 512 + (fi + 1) * 128],
                             hT[:, kc * T + tb * 512: kc * T + (tb + 1) * 512], start=(kc == 0), stop=(kc == 15))
                    k.copy('act', ob[:, tb * 512:(tb + 1) * 512], p)
                if g < 2:
                    ch = g * 4 + fi
                    k.dma('sp', S['iqT_a'][s, ch * 128:(ch + 1) * 128, 0:T], ob)
                else:
                    k.dma('sp', S['ikT_a'][s, :, 0:T], ob)
    k.barrier()


def t5_bucket_np(d):
    import math
    d = np.maximum(d, 0)
    large = 16 + (np.log(np.maximum(d, 1).astype(np.float32) / np.float32(16.0)) / np.float32(math.log(8.0))
                  * np.float32(16.0)).astype(np.int32)
    large = np.minimum(large, 31)
    return np.where(d < 16, d, large)


def host_consts_F(rel_bias):
    s = np.arange(128)[:, None, None]
    w = np.arange(2)[None, :, None]
    t = np.arange(128)[None, None, :]
    d = t - s + 128 * w
    b = t5_bucket_np(d)
    BT = rel_bias[b]
    BT = np.ascontiguousarray(np.transpose(BT, (0, 1, 3, 2))).astype(np.float32)
    tt_ = np.arange(128)[:, None]
    ss_ = np.arange(128)[None, :]
    negmask = np.where(ss_ <= tt_, 0.0, -1e30).astype(np.float32)
    return BT, negmask


def vmax8(k, out, in_):
    o, i = out.ap, in_.ap
    k.op('dve', lambda E: E.max(o, i), reads=[in_], writes=[out])


def vmatch(k, out, m8, in_, imm):
    o, m, i = out.ap, m8.ap, in_.ap
    k.op('dve', lambda E: E.match_replace(o, m, i, imm), reads=[m8, in_], writes=[out])


def interleave(gens):
    gens = [g for g in gens if g is not None]
    while gens:
        nxt = []
        for g in gens:
            try:
                next(g)
                nxt.append(g)
            except StopIteration:
                pass
        gens = nxt


def phase_attnF(k, c, S, BT_d, negmask_d, rel_bias, nseq, T=2048, TOPK=256):
    NT = T // 128
    ident = c['ident']
    SCALE = 128.0 ** -0.5
    NEG = -1e30
    EB = k.sb('EB', 2 * 16 * 128, BF16)
    negm = k.sb('negm', 128, F32)
    k.dma('sp', negm, negmask_d)
    onesc = k.sb('onesc', 1, BF16)
    k.memset('dve', onesc, 1.0)
    mark = k.sb_ptr
    BTs = k.sb('BTs', 2 * 16 * 128, F32)
    b31 = load_rowb(k, 'b31', rel_bias[31, :], n=16)
    k.dma('sp', BTs, BT_d.rearrange("s w h t -> s (w h t)"))
    BTv = BTs.rr("p (w h t) -> p w h t", w=2, h=16)
    for h in range(16):
        k.ts('dve', BTv[:, :, h, :], BTv[:, :, h, :], b31[:, h:h + 1], ALU.subtract)
    k.act(EB, BTs, AF.Exp)
    k.barrier_light = None
    k.barrier_keep = True
    _full_barrier_keep(k)
    k.sb_ptr = mark
    EBv = EB.rr("p (w h t) -> p w h t", w=2, h=16)

    kT = k.sb('kT', T, BF16)
    Vt = k.sb('Vt', T, BF16)
    ikT = k.sb('ikT', T, BF16)
    iqT = k.sb('iqT', 8 * T, BF16)
    NB3 = 3
    qTi = [k.sb('qTi%d' % i, 2048, BF16) for i in range(NB3)]
    iwi = [k.sb('iwi%d' % i, 16, F32) for i in range(NB3)]
    acc = [k.sb('acc%d' % i, T, F32) for i in range(NB3)]
    wk = [k.sb('wk%d' % i, T, F32) for i in range(2)]
    m8 = [k.sb('m8%d' % i, 8, F32) for i in range(4)]
    Rt = [k.sb('Rt%d' % i, 512, F32) for i in range(3)]
    Mt = [k.sb('Mt%d' % i, T, BF16) for i in range(2)]
    MT = [k.sb('MT%d' % i, T, BF16) for i in range(2)]
    Et = [k.sb('Et%d' % i, 512, BF16) for i in range(4)]
    E2 = [k.sb('E2%d' % i, 512, BF16) for i in range(4)]
    EMt = [k.sb('EMt%d' % i, 512, BF16) for i in range(4)]
    rden = [k.sb('rden%d' % i, 4, F32) for i in range(2)]
    ao = [k.sb('ao%d' % i, 2048, BF16) for i in range(2)]
    psS = [k.ps('psS%d' % i, i) for i in range(2)]
    psL = [k.ps('psL%d' % i, 2 + i) for i in range(2)]
    psO = [k.ps('psO%d' % i, 4 + i) for i in range(2)]
    psD = [k.pss(6, i * 8, 4) for i in range(2)]
    psT = [k.pss(7, i * 256, 256, BF16) for i in range(2)]
    cnt = {'s': 0, 'r': 0, 'l': 0, 'e': 0, 'o': 0, 't': 0, 'm8': 0}

    for s in range(nseq):
        k.dma('sp', kT, S['kT_a'][s, :, 0:T])
        k.dma('sp', Vt.rr("p (j d) -> p j d", d=128), S['v_a'][s, 0:T, :].rearrange("(j p) d -> p j d", p=128))
        k.dma('sp', ikT, S['ikT_a'][s, :, 0:T])
        k.dma('sp', iqT.rr("p (c t) -> p c t", c=8), S['iqT_a'][s, :, 0:T].rearrange("(c p) t -> p c t", p=128))

        def gen_scores(i):
            Sc = (i + 1) * 128
            q = qTi[i % NB3]; iw = iwi[i % NB3]; a = acc[i % NB3]
            k.dma('sp', q, S['qT_a'][s, i, :, :])
            k.dma('sp', iw, S['iw_a'][s, i * 128:(i + 1) * 128, :])
            for kc in range(0, Sc, 512):
                n = min(512, Sc - kc)
                for h in range(16):
                    po = (h % 2) * 64
                    ps_ = psS[cnt['s'] % 2]; cnt['s'] += 1
                    R = Rt[cnt['r'] % 3]; cnt['r'] += 1
                    k.mm(ps_[:, 0:n], iqT[po:po + 64, (h // 2) * T + i * 128:(h // 2) * T + (i + 1) * 128],
                         ikT[po:po + 64, kc:kc + n])
                    k.act(R[:, 0:n], ps_[:, 0:n], AF.Relu)
                    if h == 0:
                        k.ts(FMA_ENG, a[:, kc:kc + n], R[:, 0:n], iw[:, 0:1], ALU.mult)
                    else:
                        k.stt(FMA_ENG, a[:, kc:kc + n], R[:, 0:n], iw[:, h:h + 1], a[:, kc:kc + n], ALU.mult, ALU.add)
                    yield
            k.tt('dve', a[:, i * 128:(i + 1) * 128], a[:, i * 128:(i + 1) * 128], negm, ALU.add)
            yield

        def gen_topk(i):
            Sc = (i + 1) * 128
            a = acc[i % NB3]; M = Mt[i % 2]; mt = MT[i % 2]
            if Sc > TOPK:
                w = wk[i % 2]
                k.copy('pool', w[:, 0:Sc], a[:, 0:Sc])
                yield
                mm8 = None
                for r in range(TOPK // 8 if not ABL_TOPK else 1):
                    mm8 = m8[cnt['m8'] % 4]; cnt['m8'] += 1
                    vmax8(k, mm8, w[:, 0:Sc])
                    yield
                    if r < TOPK // 8 - 1 and not ABL_TOPK:
                        vmatch(k, w[:, 0:Sc], mm8, w[:, 0:Sc], NEG)
                        yield
                k.ts('dve', M[:, 0:Sc], a[:, 0:Sc], mm8[:, 7:8], ALU.is_ge)
            else:
                k.ts('dve', M[:, 0:Sc], a[:, 0:Sc], -1e29, ALU.is_ge)
            yield
            for j0 in range(0, i + 1, 4):
                nj = min(4, i + 1 - j0)
                pt = psT[cnt['t'] % 2]; cnt['t'] += 1
                for jj in range(nj):
                    j = j0 + jj
                    k.tr(pt[:, jj * 128:(jj + 1) * 128], M[:, j * 128:(j + 1) * 128], ident)
                k.copy('act', mt[:, j0 * 128:(j0 + nj) * 128], pt[:, 0:nj * 128])
                yield

        def gen_attn(i):
            q = qTi[i % NB3]; mt = MT[i % 2]; out = ao[i % 2]
            items = [(g, j) for g in range(4) for j in range(i + 1)]

            def front(g, j):
                pl = psL[cnt['l'] % 2]; cnt['l'] += 1
                e = Et[cnt['e'] % 4]; e2 = E2[cnt['e'] % 4]; em = EMt[cnt['e'] % 4]; cnt['e'] += 1
                k.mm(pl, kT[:, j * 128:(j + 1) * 128], q[:, g * 512:(g + 1) * 512])
                k.act(e, pl, AF.Exp, scale=SCALE)
                src = e
                if j >= i - 1:
                    wdx = i - j
                    k.tt('pool', e2.rr("p (h t) -> p h t", h=4), e.rr("p (h t) -> p h t", h=4),
                         EBv[:, wdx, g * 4:(g + 1) * 4, :], ALU.mult)
                    src = e2
                return src, em

            nxt = front(*items[0])
            grp = None
            for idx, (g, j) in enumerate(items):
                src, em = nxt
                if idx + 1 < len(items):
                    nxt = front(*items[idx + 1])
                if j == 0:
                    grp = (psO[cnt['o'] % 2], psD[cnt['o'] % 2], rden[cnt['o'] % 2]); cnt['o'] += 1
                po_, pd, rd = grp
                for h in range(4):
                    k.tt('dve' if h % 2 == 0 else 'pool', em[:, h * 128:(h + 1) * 128], src[:, h * 128:(h + 1) * 128],
                         mt[:, j * 128:(j + 1) * 128], ALU.mult)
                for h in range(4):
                    k.mm(po_[:, h * 128:(h + 1) * 128], em[:, h * 128:(h + 1) * 128], Vt[:, j * 128:(j + 1) * 128],
                         start=(j == 0 and h == 0), stop=(j == i), skip=True)
                    k.mm(pd[:, h:h + 1], em[:, h * 128:(h + 1) * 128], onesc, start=(j == 0 and h == 0), stop=(j == i),
                         skip=True)
                if j == i:
                    recip(k, 'dve', rd, pd)
                    for h in range(4):
                        hh = g * 4 + h
                        k.ts('dve', out[:, hh * 128:(hh + 1) * 128], po_[:, h * 128:(h + 1) * 128], rd[:, h:h + 1],
                             ALU.mult)
                yield
            k.dma('sp', S['yg'][s, i * 128:(i + 1) * 128, :], out)
            yield

        for step in range(NT + 2):
            interleave([gen_scores(step) if step < NT else None,
                        gen_topk(step - 1) if 0 <= step - 1 < NT else None,
                        gen_attn(step - 2) if (0 <= step - 2 < NT and not ABL_ATTN) else None])
    k.barrier()


def _full_barrier_keep(k):
    ptr = k.sb_ptr
    pd = k.phase_dsems
    k.phase_dsems = []
    k.barrier()
    k.phase_dsems = pd
    k.sb_ptr = ptr


NSEQ_CORE = 2
TSEQ = 2048
NCORES = 8


def rot_tables(T=2048):
    half = 128
    inv = (10000.0 ** (-np.arange(half, dtype=np.float32) / half)).astype(np.float32)
    ang = (np.arange(T, dtype=np.float32)[:, None] * inv[None, :]).astype(np.float32)
    return np.cos(ang).T.copy().astype(np.float32), np.sin(ang).T.copy().astype(np.float32)


def build_program():
    nc = bass.Bass("TRN2", target_bir_lowering=False)
    NS, T = NSEQ_CORE, TSEQ
    NT = T // 128

    def din(name, shape, dt=F32):
        return nc.dram_tensor(name, shape, dt, kind="ExternalInput").ap()

    def dscr(name, shape, dt=BF16):
        return nc.dram_tensor(name, shape, dt, kind="Internal").ap()

    x = din("x", [NS * T, 2048])
    cv = din("c", [NS, 2048])
    ada_w = din("ada_w", [2, 2048, 12288])
    ada_b = din("ada_b", [2, 12288])
    norm1_g = din("norm1_g", [2, 2048])
    norm2_g = din("norm2_g", [2, 2048])
    mlp_w1 = din("mlp_w1", [2, 2048, 8192])
    mlp_w2 = din("mlp_w2", [2, 8192, 2048])
    even_w_in = din("even_w_in", [2048, 8200])
    even_conv_w = din("even_conv_w", [4, 2048])
    even_gate_b = din("even_gate_b", [8])
    even_hng = din("even_head_norm_g", [8, 256])
    even_w_out = din("even_w_out", [2048, 2048])
    odd_w_in = din("odd_w_in", [2048, 3408])
    odd_qng = din("odd_q_norm_g", [128])
    odd_kng = din("odd_k_norm_g", [128])
    odd_w_out = din("odd_w_out", [2048, 2048])
    rel_bias = din("rel_bias", [32, 16])
    ident = din("ident", [128, 128])
    rot_cos = din("rot_cos", [128, 2048])
    rot_sin = din("rot_sin", [128, 2048])
    maskc = din("maskc", [5, 128, 128])
    retc = din("retc", [128, 8])
    BT = din("BT", [128, 2, 16, 128])
    negmask = din("negmask", [128, 128])
    out = nc.dram_tensor("out", [NS * T, 2048], F32, kind="ExternalOutput").ap()

    modrow = dscr("modrow", [2, NS, 12288], F32)
    xa = dscr("xa", [NS * T, 2048], F32)
    xb = dscr("xb", [NS * T, 2048], F32)
    xc = dscr("xc", [NS * T, 2048], F32)
    S = {}
    S['qkT_m'] = dscr("qkT_m", [NS, 2048, T])
    S['qT_r'] = dscr("qT_r", [NS, 1024, T])
    S['kT_r'] = dscr("kT_r", [NS, 1024, T])
    for n in ('v_m', 'og', 'v_r', 'gg'):
        S[n] = dscr(n, [NS, T, 1024])
    S['gates'] = dscr("gates", [NS, 8, T], F32)
    S['yg'] = dscr("yg", [NS, T, 2048])
    S['qT_a'] = dscr("qT_a", [NS, NT, 128, 2048])
    S['kT_a'] = dscr("kT_a", [NS, 128, T])
    S['v_a'] = dscr("v_a", [NS, T, 128])
    S['iw_a'] = dscr("iw_a", [NS, T, 16], F32)
    S['iqT_a'] = dscr("iqT_a", [NS, 1024, T])
    S['ikT_a'] = dscr("ikT_a", [NS, 128, T])
    ygf = S['yg'].rearrange("s t d -> (s t) d")

    with ExitStack() as es:
        k = K(nc, es)
        c = setup_consts(k, ident)
        G = {}
        G['identF'] = k.sb('identF', 128, F32)
        k.dma('sp', G['identF'], ident)
        for n in ('ea_col', 'xi_col', 'kap_col'):
            G[n] = k.sb(n, 16 * 4 * NS, F32)
        k.persist()
        phase_mod(k, c, cv, ada_w, ada_b, modrow, NS, nlayers=2)
        phase_projA(k, c, x, even_w_in, even_conv_w, modrow[0], norm1_g[0], rot_cos, rot_sin, S, NS, T)
        phase_gatesB(k, c, S, even_gate_b, G, NS, T)
        phase_recB(k, c, S, G, even_hng, maskc, retc, NS, T)
        phase_outproj(k, c, ygf, even_w_out, x, xa, modrow[0], NS, T)
        phase_mlp(k, c, xa, xb, mlp_w1[0], mlp_w2[0], modrow[0], norm2_g[0], NS, T)
        phase_projE(k, c, xb, odd_w_in, modrow[1], norm1_g[1], odd_qng, odd_kng, S, NS, T)
        phase_attnF(k, c, S, BT, negmask, rel_bias, NS, T)
        phase_outproj(k, c, ygf, odd_w_out, xb, xc, modrow[1], NS, T)
        phase_mlp(k, c, xc, out, mlp_w1[1], mlp_w2[1], modrow[1], norm2_g[1], NS, T)
        k.emit()
    return nc


_PROG = {}


def kernel(x, c, ada_w, ada_b, norm1_g, norm2_g, mlp_w1, mlp_w2,
           even_w_in, even_conv_w, even_gate_b, even_head_norm_g, even_w_out,
           odd_w_in, odd_q_norm_g, odd_k_norm_g, odd_w_out, rel_bias):
    f = lambda a: np.ascontiguousarray(np.asarray(a, dtype=np.float32))
    if 'nc' not in _PROG:
        _PROG['nc'] = build_program()
    nc = _PROG['nc']
    x = f(x); c = f(c)
    rel_bias = f(rel_bias)
    cosT, sinT = rot_tables()
    mc, rcn = host_consts_B()
    BT, negmask = host_consts_F(rel_bias)
    shared = {
        "ada_w": f(ada_w), "ada_b": f(ada_b), "norm1_g": f(norm1_g), "norm2_g": f(norm2_g),
        "mlp_w1": f(mlp_w1), "mlp_w2": f(mlp_w2),
        "even_w_in": f(even_w_in)[0], "even_conv_w": f(even_conv_w)[0], "even_gate_b": f(even_gate_b)[0],
        "even_head_norm_g": f(even_head_norm_g)[0], "even_w_out": f(even_w_out)[0],
        "odd_w_in": f(odd_w_in)[0], "odd_q_norm_g": f(odd_q_norm_g)[0], "odd_k_norm_g": f(odd_k_norm_g)[0],
        "odd_w_out": f(odd_w_out)[0], "rel_bias": rel_bias,
        "ident": np.eye(128, dtype=np.float32), "rot_cos": cosT, "rot_sin": sinT,
        "maskc": mc, "retc": rcn, "BT": BT, "negmask": negmask,
    }
    in_maps = []
    for i in range(NCORES):
        m = dict(shared)
        m["x"] = np.ascontiguousarray(x[i * NSEQ_CORE:(i + 1) * NSEQ_CORE].reshape(NSEQ_CORE * TSEQ, 2048))
        m["c"] = np.ascontiguousarray(c[i * NSEQ_CORE:(i + 1) * NSEQ_CORE])
        in_maps.append(m)
    res = run_bass_kernel_spmd(nc, in_maps, core_ids=list(range(NCORES)))
    outs = [np.asarray(r["out"], dtype=np.float32).reshape(NSEQ_CORE, TSEQ, 2048) for r in res.results]
    return np.concatenate(outs, axis=0)
```
